# Optimizing a Trainium2 kernel written in Bass

```python
import math
import jax, jax.numpy as jnp
from jax import lax
import numpy as np

D_MODEL = 2048
BATCH = 1
SEQ = 8192
DEPTH = 4
DEC_BATCH = 1
DEC_SEQ = 16384
PAST_LEN = 128

N_MIXERS = 2
N_SSM_LAYERS = (DEPTH + 1) // 2
N_ATTN_LAYERS = DEPTH // 2
PLE_DIM = 256
GRID_W = 64
SSM_WIDTH = D_MODEL
SSM_GROUP = 16
SSM_GROUPS = SSM_WIDTH // SSM_GROUP
SSM_STATE = 64
SCAN_CHUNK = 128
DT_MIN = 0.001
DT_MAX = 0.1
HEAD_DIM = 128
N_HEADS = D_MODEL // HEAD_DIM
WIN_H = 8
WIN_W = 16
D_FF = -(-8 * D_MODEL // (3 * 256)) * 256
EPS = 1e-6

kernel_name = "hybrid_s5_natten_encoder"


def _rmsnorm(x, g):
    xf = x.astype(jnp.float32)
    y = xf * lax.rsqrt(jnp.mean(xf * xf, axis=-1, keepdims=True) + EPS)
    return (y * g.astype(jnp.float32)).astype(x.dtype)


def _complex_affine_combine(e1, e2):
    a1r, a1i, b1r, b1i = e1
    a2r, a2i, b2r, b2i = e2
    ar = a2r * a1r - a2i * a1i
    ai = a2r * a1i + a2i * a1r
    br = a2r * b1r - a2i * b1i + b2r
    bi = a2r * b1i + a2i * b1r + b2i
    return (ar, ai, br, bi)


def _s5_scan(u, a_re, a_im, log_dt, b_re, b_im, c_re, c_im):
    f32 = jnp.float32
    a_re, a_im = a_re.astype(f32), a_im.astype(f32)
    b_re, b_im = b_re.astype(f32), b_im.astype(f32)
    c_re, c_im = c_re.astype(f32), c_im.astype(f32)
    dt = jnp.exp(log_dt.astype(f32))[:, None]
    mag = jnp.exp(a_re * dt)
    lam_re = mag * jnp.cos(a_im * dt)
    lam_im = mag * jnp.sin(a_im * dt)
    den = a_re * a_re + a_im * a_im
    xr, xi = lam_re - 1.0, lam_im
    f_re = (xr * a_re + xi * a_im) / den
    f_im = (xi * a_re - xr * a_im) / den
    bb_re = f_re[..., None] * b_re - f_im[..., None] * b_im
    bb_im = f_re[..., None] * b_im + f_im[..., None] * b_re

    bsz, seq = u.shape[0], u.shape[1]
    n_chunks = seq // SCAN_CHUNK
    uc = jnp.moveaxis(u.reshape(bsz, n_chunks, SCAN_CHUNK, SSM_GROUPS, SSM_GROUP), 1, 0)
    shp = (bsz, SCAN_CHUNK, SSM_GROUPS, SSM_STATE)
    lam_seq_re = jnp.broadcast_to(lam_re, shp)
    lam_seq_im = jnp.broadcast_to(lam_im, shp)

    def step(carry, u_blk):
        h_re, h_im = carry
        bu_re = jnp.einsum('btgc,gpc->btgp', u_blk, bb_re)
        bu_im = jnp.einsum('btgc,gpc->btgp', u_blk, bb_im)
        pw_re, pw_im, x_re, x_im = lax.associative_scan(
            _complex_affine_combine, (lam_seq_re, lam_seq_im, bu_re, bu_im), axis=1)
        s_re = x_re + pw_re * h_re[:, None] - pw_im * h_im[:, None]
        s_im = x_im + pw_re * h_im[:, None] + pw_im * h_re[:, None]
        y = (jnp.einsum('btgp,gcp->btgc', s_re, c_re)
             - jnp.einsum('btgp,gcp->btgc', s_im, c_im))
        return (s_re[:, -1], s_im[:, -1]), y

    init = (jnp.zeros((bsz, SSM_GROUPS, SSM_STATE), f32),
            jnp.zeros((bsz, SSM_GROUPS, SSM_STATE), f32))
    _, ys = lax.scan(step, init, uc)
    return jnp.moveaxis(ys, 0, 1).reshape(bsz, seq, SSM_GROUPS, SSM_GROUP)


def _s5_mixer(x, w_in, a_re, a_im, log_dt, b_re, b_im, c_re, c_im, d, w_glu):
    bsz, seq, _ = x.shape
    u = (x @ w_in).astype(jnp.float32)
    ug = u.reshape(bsz, seq, SSM_GROUPS, SSM_GROUP)
    y_f = _s5_scan(ug, a_re[0], a_im[0], log_dt[0], b_re[0], b_im[0], c_re[0], c_im[0])
    y_b = jnp.flip(_s5_scan(jnp.flip(ug, axis=1), a_re[1], a_im[1], log_dt[1],
                            b_re[1], b_im[1], c_re[1], c_im[1]), axis=1)
    y = (y_f + y_b).reshape(bsz, seq, SSM_WIDTH) + d.astype(jnp.float32) * u
    g = jax.nn.gelu(y).astype(x.dtype)
    ab = g @ w_glu
    return ab[..., :D_MODEL] * jax.nn.sigmoid(ab[..., D_MODEL:])


def _head_rmsnorm(t, g):
    tf = t.astype(jnp.float32)
    y = tf * lax.rsqrt(jnp.mean(tf * tf, axis=-1, keepdims=True) + EPS)
    return (y * g.astype(jnp.float32)).astype(t.dtype)


def _neighborhood_attention(x, w_qkv, q_gain, k_gain, rpb, w_o):
    bsz, seq, _ = x.shape
    rows = seq // GRID_W
    kh = min(WIN_H, rows)
    qkv = (x @ w_qkv).reshape(bsz, seq, 3, N_HEADS, HEAD_DIM)
    q = _head_rmsnorm(qkv[:, :, 0], q_gain) * (HEAD_DIM ** -0.5)
    k = _head_rmsnorm(qkv[:, :, 1], k_gain)
    v = qkv[:, :, 2]
    q_grid = q.reshape(bsz, rows, GRID_W, N_HEADS, HEAD_DIM)
    k_grid = k.reshape(bsz, rows, GRID_W, N_HEADS, HEAD_DIM)
    v_grid = v.reshape(bsz, rows, GRID_W, N_HEADS, HEAD_DIM)

    cols = jnp.arange(GRID_W)
    col_start = jnp.clip(cols - WIN_W // 2, 0, GRID_W - WIN_W)
    col_idx = col_start[:, None] + jnp.arange(WIN_W)[None, :]
    col_off = col_idx - cols[:, None] + (WIN_W - 1)
    rpb_cols = rpb[:, :, col_off]

    def row_block(r):
        rs = jnp.clip(r - kh // 2, 0, rows - kh)
        q_r = lax.dynamic_index_in_dim(q_grid, r, axis=1, keepdims=False)
        k_rows = lax.dynamic_slice_in_dim(k_grid, rs, kh, axis=1)
        v_rows = lax.dynamic_slice_in_dim(v_grid, rs, kh, axis=1)
        k_win = k_rows[:, :, col_idx]
        v_win = v_rows[:, :, col_idx]
        row_off = rs + jnp.arange(kh) - r + (WIN_H - 1)
        bias = jnp.take(rpb_cols, row_off, axis=1)
        bias = jnp.transpose(bias, (0, 2, 1, 3)).astype(jnp.float32)
        s = jnp.einsum('bqhd,brqkhd->bhqrk', q_r, k_win).astype(jnp.float32) + bias[None]
        p = jax.nn.softmax(s.reshape(bsz, N_HEADS, GRID_W, kh * WIN_W), axis=-1)
        p = p.reshape(bsz, N_HEADS, GRID_W, kh, WIN_W).astype(v.dtype)
        return jnp.einsum('bhqrk,brqkhd->bqhd', p, v_win)

    out = lax.map(row_block, jnp.arange(rows))
    out = jnp.moveaxis(out, 0, 1).reshape(bsz, seq, N_HEADS * HEAD_DIM)
    return out @ w_o


def _swiglu(x, w_gate, w_up, w_down):
    return (jax.nn.silu(x @ w_gate) * (x @ w_up)) @ w_down


def _trunk(x, p, norm_mix, norm_ffn, norm_ple, s5_w_in, s5_a_re, s5_a_im, s5_log_dt,
           s5_b_re, s5_b_im, s5_c_re, s5_c_im, s5_d, s5_w_glu, attn_w_qkv, attn_q_norm,
           attn_k_norm, attn_rpb, attn_w_o, ffn_w_gate, ffn_w_up, ffn_w_down,
           ple_w_gate, ple_w_proj):
    h = x
    for i in range(DEPTH):
        j = i // N_MIXERS
        hn = _rmsnorm(h, norm_mix[i])
        if i % N_MIXERS == 0:
            mix = _s5_mixer(hn, s5_w_in[j], s5_a_re[j], s5_a_im[j], s5_log_dt[j],
                            s5_b_re[j], s5_b_im[j], s5_c_re[j], s5_c_im[j], s5_d[j], s5_w_glu[j])
        else:
            mix = _neighborhood_attention(hn, attn_w_qkv[j], attn_q_norm[j], attn_k_norm[j],
                                          attn_rpb[j], attn_w_o[j])
        h = h + mix.astype(h.dtype)
        h = h + _swiglu(_rmsnorm(h, norm_ffn[i]), ffn_w_gate[i], ffn_w_up[i], ffn_w_down[i])
        gate = jax.nn.sigmoid(_rmsnorm(h, norm_ple[i]) @ ple_w_gate[i])
        h = h + gate * (p[i] @ ple_w_proj[i])
    return h


def setup_inputs(seed: int = 0) -> dict:
    key = jax.random.key(seed)
    ks = jax.random.split(key, 32)
    f32 = jnp.float32

    def nrm(k, shape, scale):
        return jax.random.normal(k, shape, f32) * scale

    ns, na = N_SSM_LAYERS, N_ATTN_LAYERS
    G, P, GC = SSM_GROUPS, SSM_STATE, SSM_GROUP
    a_im_base = math.pi * jnp.arange(P, dtype=f32)
    return {
        "x_prompt": nrm(ks[0], (BATCH, SEQ, D_MODEL), 1.0),
        "x_sample": nrm(ks[1], (DEC_BATCH, DEC_SEQ, D_MODEL), 1.0),
        "p_prompt": nrm(ks[2], (DEPTH, BATCH, SEQ, PLE_DIM), 1.0),
        "p_sample": nrm(ks[3], (DEPTH, DEC_BATCH, DEC_SEQ, PLE_DIM), 1.0),
        "norm_mix": 1.0 + nrm(ks[4], (DEPTH, D_MODEL), 0.02),
        "norm_ffn": 1.0 + nrm(ks[5], (DEPTH, D_MODEL), 0.02),
        "norm_ple": 1.0 + nrm(ks[6], (DEPTH, D_MODEL), 0.02),
        "s5_w_in": nrm(ks[7], (ns, D_MODEL, SSM_WIDTH), D_MODEL ** -0.5),
        "s5_a_re": -0.5 * jnp.exp(nrm(ks[8], (ns, 2, G, P), 0.05)),
        "s5_a_im": a_im_base + nrm(ks[9], (ns, 2, G, P), 0.01),
        "s5_log_dt": jax.random.uniform(ks[10], (ns, 2, G), f32,
                                        math.log(DT_MIN), math.log(DT_MAX)),
        "s5_b_re": nrm(ks[11], (ns, 2, G, P, GC), (2 * GC) ** -0.5),
        "s5_b_im": nrm(ks[12], (ns, 2, G, P, GC), (2 * GC) ** -0.5),
        "s5_c_re": nrm(ks[13], (ns, 2, G, GC, P), (2 * P) ** -0.5),
        "s5_c_im": nrm(ks[14], (ns, 2, G, GC, P), (2 * P) ** -0.5),
        "s5_d": 1.0 + nrm(ks[15], (ns, SSM_WIDTH), 0.1),
        "s5_w_glu": nrm(ks[16], (ns, SSM_WIDTH, 2 * D_MODEL), SSM_WIDTH ** -0.5),
        "attn_w_qkv": nrm(ks[17], (na, D_MODEL, 3 * N_HEADS * HEAD_DIM), D_MODEL ** -0.5),
        "attn_q_norm": 1.0 + nrm(ks[18], (na, HEAD_DIM), 0.02),
        "attn_k_norm": 1.0 + nrm(ks[19], (na, HEAD_DIM), 0.02),
        "attn_rpb": nrm(ks[20], (na, N_HEADS, 2 * WIN_H - 1, 2 * WIN_W - 1), 0.1),
        "attn_w_o": nrm(ks[21], (na, N_HEADS * HEAD_DIM, D_MODEL), D_MODEL ** -0.5),
        "ffn_w_gate": nrm(ks[22], (DEPTH, D_MODEL, D_FF), D_MODEL ** -0.5),
        "ffn_w_up": nrm(ks[23], (DEPTH, D_MODEL, D_FF), D_MODEL ** -0.5),
        "ffn_w_down": nrm(ks[24], (DEPTH, D_FF, D_MODEL), D_FF ** -0.5),
        "ple_w_gate": nrm(ks[25], (DEPTH, D_MODEL, D_MODEL), D_MODEL ** -0.5),
        "ple_w_proj": nrm(ks[26], (DEPTH, PLE_DIM, D_MODEL), PLE_DIM ** -0.5),
    }


def reference(x_prompt, x_sample, p_prompt, p_sample, norm_mix, norm_ffn, norm_ple,
              s5_w_in, s5_a_re, s5_a_im, s5_log_dt, s5_b_re, s5_b_im, s5_c_re, s5_c_im,
              s5_d, s5_w_glu, attn_w_qkv, attn_q_norm, attn_k_norm, attn_rpb, attn_w_o,
              ffn_w_gate, ffn_w_up, ffn_w_down, ple_w_gate, ple_w_proj):
    y_prompt = _trunk(x_prompt, p_prompt, norm_mix, norm_ffn, norm_ple, s5_w_in, s5_a_re,
                      s5_a_im, s5_log_dt, s5_b_re, s5_b_im, s5_c_re, s5_c_im, s5_d, s5_w_glu,
                      attn_w_qkv, attn_q_norm, attn_k_norm, attn_rpb, attn_w_o,
                      ffn_w_gate, ffn_w_up, ffn_w_down, ple_w_gate, ple_w_proj)
    y_sample = _trunk(x_sample, p_sample, norm_mix, norm_ffn, norm_ple, s5_w_in, s5_a_re,
                      s5_a_im, s5_log_dt, s5_b_re, s5_b_im, s5_c_re, s5_c_im, s5_d, s5_w_glu,
                      attn_w_qkv, attn_q_norm, attn_k_norm, attn_rpb, attn_w_o,
                      ffn_w_gate, ffn_w_up, ffn_w_down, ple_w_gate, ple_w_proj)
    return (y_prompt, y_sample)
```

```python
import math
from contextlib import ExitStack
import numpy as np
import ml_dtypes
import concourse.bass as bass
import concourse.mybir as mybir
from concourse.bass_utils import run_bass_kernel_spmd

F32 = mybir.dt.float32
BF16 = mybir.dt.bfloat16
AF = mybir.ActivationFunctionType
ALU = mybir.AluOpType

D = 2048
KC = 16
T = 512
DFF = 5632
JC = DFF // 128
PLE = 256
NH = 16
GRID_W = 64
NTT = 128
MAGIC = 12582912.0
TWO_PI = 2.0 * math.pi
EPS = 1e-6
NEG = -30000.0
ENGS = ("pe", "act", "dve", "pool", "sp")
ENGMAP = {"pe": "tensor", "act": "scalar", "dve": "vector", "pool": "gpsimd", "sp": "sync"}


class Buf:
    def __init__(self, name="b", excl=False):
        self.name = name
        self.last_write = None
        self.readers = []
        self.excl = excl


def PBuf(name="p"):
    return Buf(name, excl=True)


class Prog:
    def __init__(self, nc, stack):
        self.nc = nc
        self.stack = stack
        self.ops = {e: [] for e in ENGS}
        self.count = {e: 0 for e in ENGS}
        self.dma_cnt = {}
        self.sems = {}
        for e in ENGS:
            self.sems[("eng", e)] = stack.enter_context(nc.semaphore("s_" + e))
        self.n_inst = 0

    def _sem(self, kind, key):
        k = (kind, key)
        if k not in self.sems:
            self.sems[k] = self.stack.enter_context(self.nc.semaphore("d_" + key))
        return self.sems[k]

    def _waits_for(self, reads, writes):
        hs = []
        for b in reads:
            if b.last_write is not None:
                hs.append(b.last_write)
            if b.excl:
                hs.extend(b.readers)
        for b in writes:
            if b.last_write is not None:
                hs.append(b.last_write)
            hs.extend(b.readers)
        best = {}
        for (kind, key, val) in hs:
            k = (kind, key)
            if k not in best or best[k] < val:
                best[k] = val
        return [(k[0], k[1], v) for k, v in best.items()]

    def _mark(self, h, reads, writes):
        for b in writes:
            b.last_write = h
            b.readers = []
        for b in reads:
            if b not in writes:
                b.readers.append(h)

    def op(self, eng, fn, reads=(), writes=()):
        waits = self._waits_for(reads, writes)
        if eng == "pe":
            waits = [w for w in waits if not (w[0] == "eng" and w[1] == "pe")]
        self.count[eng] += 1
        h = ("eng", eng, self.count[eng])
        self.ops[eng].append((waits, fn, ("eng", eng, 1)))
        self._mark(h, reads, writes)
        self.n_inst += 1
        return h

    def dma(self, queue, fn, semname, reads=(), writes=(), inc=16):
        waits = self._waits_for(reads, writes)
        self._sem("dma", semname)
        self.dma_cnt[semname] = self.dma_cnt.get(semname, 0) + inc
        h = ("dma", semname, self.dma_cnt[semname])
        self.ops[queue].append((waits, fn, ("dma", semname, inc)))
        self._mark(h, reads, writes)
        self.n_inst += 1
        return h

    def end_phase(self):
        waits = [("dma", name, cnt) for name, cnt in self.dma_cnt.items()]
        self.ops["sp"].append((waits, None, None))
        nc = self.nc
        sems = self.sems
        with nc.Block() as block:
            def make(e):
                oplist = self.ops[e]

                def body(engobj):
                    for (waits, fn, inc) in oplist:
                        for (kind, key, val) in waits:
                            engobj.wait_ge(sems[(kind, key)], val)
                        if fn is not None:
                            ins = fn(engobj)
                            if inc[0] == "dma" and inc[2] == 1:
                                ins.then_inc(sems[(inc[0], inc[1])])
                            else:
                                ins.then_inc(sems[(inc[0], inc[1])], inc[2])
                return body
            for e in ENGS:
                if self.ops[e]:
                    getattr(block, ENGMAP[e])(make(e))
        self.ops = {e: [] for e in ENGS}


class Cfg:
    def __init__(self, ncores, R0, R1):
        self.nc_ = ncores
        self.R = (R0, R1)
        self.N = (R0 * GRID_W, R1 * GRID_W)
        self.base = (0, self.N[0])
        self.ntok = self.N[0] + self.N[1]
        self.nch = (self.N[0] // T, self.N[1] // T)
        self.tiles = []
        for s in range(2):
            for c in range(self.nch[s]):
                self.tiles.append((s, c, self.base[s] + c * T))


class Builder:
    def __init__(self, cfg):
        self.cfg = cfg
        nc = self.nc = bass.Bass("TRN2", target_bir_lowering=False)
        self.outer = ExitStack()
        self.P = Prog(nc, self.outer)
        ntok = cfg.ntok
        NC = cfg.nc_

        import os
        only = os.environ.get("MK_ONLY")
        only = set(only.split(",")) if only else None
        self.ext_inputs = []

        def din(name, shape, dt=F32):
            if only is not None and name not in only:
                return nc.dram_tensor(name, shape, dt).ap()
            self.ext_inputs.append(name)
            return nc.dram_tensor(name, shape, dt, kind="ExternalInput").ap()

        def dscr(name, shape, dt=F32):
            return nc.dram_tensor(name, shape, dt)

        self.xT = din("xT", [D, ntok])
        self.pT = din("pT", [4, PLE, ntok])
        self.gmix = din("gmix", [128, 4, KC]); self.gffn = din("gffn", [128, 4, KC]); self.gple = din("gple", [128, 4, KC])
        self.w_in = din("s5_w_in", [2, D, D]); self.w_glu = din("s5_w_glu", [2, D, 2 * D])
        self.A_re = din("A_re", [2, 128, NTT]); self.A_im = din("A_im", [2, 128, NTT]); self.LOGDT = din("LOGDT", [2, 128, NTT])
        self.B_re = din("B_re", [2, 128, NTT, 16]); self.B_im = din("B_im", [2, 128, NTT, 16])
        self.C_re = din("C_re", [2, 128, NTT, 16]); self.C_im = din("C_im", [2, 128, NTT, 16])
        self.DSK = din("DSK", [2, 128, KC])
        self.w_qkv = din("attn_w_qkv", [2, D, 3 * D]); self.w_o = din("attn_w_o", [2, D, D])
        self.QG = din("QG", [2, 128, 1]); self.KG = din("KG", [2, 128, 1])
        self.BIAS_INT = din("BIAS_INT", [2, 128, NH, 5, 128]); self.BIAS_SP = din("BIAS_SP", [2, 2, 4, 128, NH, 6, 128])
        self.w_gate = din("ffn_w_gate", [4, D, DFF]); self.w_up = din("ffn_w_up", [4, D, DFF]); self.w_down = din("ffn_w_down", [4, DFF, D])
        self.w_pg = din("ple_w_gate", [4, D, D]); self.w_pp = din("ple_w_proj", [4, PLE, D])
        self.IOTA = din("IOTA", [128, T]); self.IDENT = din("IDENT", [128, 128]); self.OH = din("OH", [128, 3, NC])
        self.yT = nc.dram_tensor("yT", [D, ntok], F32, kind="ExternalOutput").ap()

        self.hT = dscr("hT", [D, ntok]).ap()
        self.u_scr = dscr("u_scr", [D, ntok]).ap()
        self.ubf_scr = dscr("ubf_scr", [D, ntok], BF16).ap()
        self.g_scr = dscr("g_scr", [D, ntok], BF16).ap()
        self.tblB = dscr("tblB", [128, 2 * NTT, 128], BF16).ap()
        self.tblC = dscr("tblC", [128, 2 * NTT, 128], BF16).ap()
        self.q_scr = dscr("q_scr", [128, NH, ntok], BF16).ap()
        self.kT_scr = [dscr(f"kT_scr{s}", [128, NH, (cfg.R[s] + 8) * GRID_W], BF16).ap() for s in range(2)]
        self.v_scr = [dscr(f"v_scr{s}", [(cfg.R[s] + 8) * GRID_W, D], BF16).ap() for s in range(2)]
        self.ccS_src = dscr("ccS_src", [128, 512]); self.ccS_dst = dscr("ccS_dst", [NC * 128, 512])
        self.ccK_src = [dscr(f"ccK_src{s}", [128, NH * 512], BF16) for s in range(2)]
        self.ccK_dst = [dscr(f"ccK_dst{s}", [NC * 128, NH * 512], BF16) for s in range(2)]
        self.ccV_src = [dscr(f"ccV_src{s}", [128, 4 * D], BF16) for s in range(2)]
        self.ccV_dst = [dscr(f"ccV_dst{s}", [NC * 128, 4 * D], BF16) for s in range(2)]

        o = self.outer
        self.s5p = o.enter_context(nc.sbuf_tensor("s5p", [128, 6, NTT], F32))
        self.ini_all = o.enter_context(nc.sbuf_tensor("ini_all", [128, 2, 2, NTT], F32))
        self.sloc = o.enter_context(nc.sbuf_tensor("sloc", [128, 2, 2, NTT], F32))
        self.consts = o.enter_context(nc.sbuf_tensor("consts", [128, 4 * KC * 3 + 2 * KC + 4 + 3 * NC], F32))
        self.iota = o.enter_context(nc.sbuf_tensor("iota", [128, T], F32))
        self.ones_f = o.enter_context(nc.sbuf_tensor("ones_f", [128, T], F32))
        self.ones_b = o.enter_context(nc.sbuf_tensor("ones_b", [128, 128], BF16))
        self.ident = o.enter_context(nc.sbuf_tensor("ident", [128, 128], F32))
        self.b_s5p = Buf("s5p"); self.b_ini = Buf("ini"); self.b_sloc = Buf("sloc"); self.b_const = Buf("const")
        c0 = 0
        self.c_gmix = self.consts[:, c0:c0 + 64]; c0 += 64
        self.c_gffn = self.consts[:, c0:c0 + 64]; c0 += 64
        self.c_gple = self.consts[:, c0:c0 + 64]; c0 += 64
        self.c_dsk = self.consts[:, c0:c0 + 32]; c0 += 32
        self.c_qg = self.consts[:, c0:c0 + 2]; c0 += 2
        self.c_kg = self.consts[:, c0:c0 + 2]; c0 += 2
        self.c_oh = self.consts[:, c0:c0 + 3 * NC]; c0 += 3 * NC

    def un(self, name):
        self._uid = getattr(self, "_uid", 0) + 1
        return f"{name}_{self._uid}"

    def dve(self, fn, r=(), w=()):
        return self.P.op("dve", fn, reads=r, writes=w)

    def act(self, fn, r=(), w=()):
        return self.P.op("act", fn, reads=r, writes=w)

    def pe(self, fn, r=(), w=()):
        return self.P.op("pe", fn, reads=r, writes=w)

    def pool(self, fn, r=(), w=()):
        return self.P.op("pool", fn, reads=r, writes=w)

    def load(self, out, in_, sem, w, r=(), queue="sp"):
        return self.P.dma(queue, lambda e: e.dma_start(out=out, in_=in_), sem, reads=r, writes=w)

    def tt(self, out, a, b, op, r, w, eng="dve"):
        return self.P.op(eng, lambda e: e.tensor_tensor(out=out, in0=a, in1=b, op=op), reads=r, writes=w)

    def phase_consts(self):
        bc = self.b_const
        L = self.load
        L(self.c_gmix.rearrange("p (l k) -> p l k", l=4), self.gmix, "c0", [bc])
        L(self.c_gffn.rearrange("p (l k) -> p l k", l=4), self.gffn, "c1", [bc])
        L(self.c_gple.rearrange("p (l k) -> p l k", l=4), self.gple, "c2", [bc])
        for j in range(2):
            L(self.c_dsk[:, j * KC:(j + 1) * KC], self.DSK[j], "c3", [bc])
            L(self.c_qg[:, j:j + 1], self.QG[j], "c4", [bc])
            L(self.c_kg[:, j:j + 1], self.KG[j], "c5", [bc])
        L(self.c_oh.rearrange("p (a n) -> p a n", a=3), self.OH, "c6", [bc])
        L(self.iota[:], self.IOTA, "c7", [bc])
        L(self.ident[:], self.IDENT, "c8", [bc])
        self.dve(lambda e: e.memset(self.ones_f[:], 1.0), w=[bc])
        self.dve(lambda e: e.memset(self.ones_b[:], 1.0), w=[bc])
        self.P.end_phase()

    def rmsnorm(self, h_sb, bh, gain, hn, bhn, sq, bsq, rstd, brstd, ps, bps):
        bc = self.b_const
        self.act(lambda e: e.activation(out=sq, in_=h_sb, func=AF.Square), r=[bh], w=[bsq])
        for k in range(KC):
            self.pe(lambda e, k=k: e.matmul(ps, lhsT=self.ones_b[:], rhs=sq[:, k, :], start=(k == 0), stop=(k == KC - 1)), r=[bsq, bc], w=[bps])
        self.act(lambda e: e.activation(out=rstd, in_=ps, func=AF.Sqrt, bias=EPS, scale=1.0 / D), r=[bps], w=[brstd])
        self.dve(lambda e: e.reciprocal(out=rstd, in_=rstd), r=[brstd], w=[brstd])
        for k in range(KC):
            self.dve(lambda e, k=k: e.scalar_tensor_tensor(out=hn[:, k, :], in0=h_sb[:, k, :], scalar=gain[:, k:k + 1], in1=rstd,
                                                           op0=ALU.mult, op1=ALU.mult), r=[bh, brstd, bc], w=[bhn])

    def wload(self, W, slot_ap, kcin, c0, ncols, slot, bw):
        src = W[:, c0:c0 + ncols].rearrange("(k p) n -> p k n", p=128)
        return self.P.dma("pool", lambda e: e.dma_start(out=slot_ap, in_=src), f"w{slot}", writes=[bw])

    def phase_rowlocal(self, layer, mixer, first_layer, last_layer):
        cfg, nc, P = self.cfg, self.nc, self.P
        j = layer // 2
        with ExitStack() as st:
            def sb(name, shape, dt=F32):
                return st.enter_context(nc.sbuf_tensor(self.un(name), shape, dt))
            h_sb = sb("h_sb", [128, KC, T]); hn = sb("hn", [128, KC, T], BF16)
            mid = sb("mid", [128, JC, T], BF16)
            gin = sb("gin", [128, KC, T], BF16)
            wbuf = [sb(f"wbuf{i}", [128, 12288], BF16) for i in range(2)]
            wpp = sb("wpp", [128, 2, D], BF16)
            p32 = sb("p32", [128, 2, T]); pbf = sb("pbf", [128, 2, T], BF16)
            tmp = [sb(f"tmp{i}", [128, T]) for i in range(2)]
            rstd = sb("rstd", [128, T])
            ps_ss = st.enter_context(nc.psum_tensor(self.un("ps_ss"), [128, T], F32))
            banks = [st.enter_context(nc.psum_tensor(self.un(f"bk{i}"), [128, T], F32)) for i in range(6)]
            bh, bhn, bmid, bgin, bwpp, bp32, bpbf, brstd = (Buf() for _ in range(8)); bpss = PBuf()
            bw = [Buf(), Buf()]; btmp = [Buf(), Buf()]; bbk = [PBuf() for _ in range(6)]
            st_ = {"w": 0, "b": 0, "t": 0}

            def nslot():
                s = st_["w"] % 2; st_["w"] += 1; return s

            def nbank():
                b = st_["b"] % 6; st_["b"] += 1; return b

            def ntmp():
                t_ = st_["t"] % 2; st_["t"] += 1; return t_

            sq = mid[:, 0:KC, :]
            src_h = self.xT if first_layer else self.hT
            dst_h = self.yT if last_layer else self.hT
            P.dma("pool", lambda e: e.dma_start(out=wpp[:], in_=self.w_pp[layer].rearrange("(k p) n -> p k n", p=128)), "wpp", writes=[bwpp])
            for (s, c, t0) in cfg.tiles:
                self.load(h_sb[:], src_h[:, t0:t0 + T].rearrange("(k p) t -> p k t", p=128), "ld_h", [bh])
                self.load(gin[:], self.g_scr[:, t0:t0 + T].rearrange("(k p) t -> p k t", p=128), "ld_g", [bgin])
                if mixer == "s5":
                    W = self.w_glu[j]
                    for cb in range(D // 256):
                        sl = nslot()
                        va = wbuf[sl][:, 0:KC * 256].rearrange("p (k n) -> p k n", k=KC)
                        vb = wbuf[sl][:, KC * 256:2 * KC * 256].rearrange("p (k n) -> p k n", k=KC)
                        self.wload(W, va, KC, cb * 256, 256, sl, bw[sl])
                        self.wload(W, vb, KC, D + cb * 256, 256, sl, bw[sl])
                        for mi in range(2):
                            m = cb * 2 + mi
                            ba, bb_ = nbank(), nbank()
                            for k in range(KC):
                                self.pe(lambda e, k=k, va=va, mi=mi, ba=ba: e.matmul(banks[ba][:], lhsT=va[:, k, mi * 128:(mi + 1) * 128], rhs=gin[:, k, :],
                                                                                     start=(k == 0), stop=(k == KC - 1)), r=[bw[sl], bgin], w=[bbk[ba]])
                            for k in range(KC):
                                self.pe(lambda e, k=k, vb=vb, mi=mi, bb_=bb_: e.matmul(banks[bb_][:], lhsT=vb[:, k, mi * 128:(mi + 1) * 128], rhs=gin[:, k, :],
                                                                                       start=(k == 0), stop=(k == KC - 1)), r=[bw[sl], bgin], w=[bbk[bb_]])
                            ti = ntmp()
                            self.act(lambda e, ti=ti, bb_=bb_: e.activation(out=tmp[ti][:], in_=banks[bb_][:], func=AF.Sigmoid), r=[bbk[bb_]], w=[btmp[ti]])
                            self.tt(tmp[ti][:], tmp[ti][:], banks[ba][:], ALU.mult, [btmp[ti], bbk[ba]], [btmp[ti]])
                            self.tt(h_sb[:, m, :], h_sb[:, m, :], tmp[ti][:], ALU.add, [bh, btmp[ti]], [bh])
                else:
                    W = self.w_o[j]
                    for cb in range(D // 512):
                        sl = nslot()
                        va = wbuf[sl][:, 0:KC * 512].rearrange("p (k n) -> p k n", k=KC)
                        self.wload(W, va, KC, cb * 512, 512, sl, bw[sl])
                        for mi in range(4):
                            m = cb * 4 + mi
                            ba = nbank()
                            for k in range(KC):
                                self.pe(lambda e, k=k, va=va, mi=mi, ba=ba: e.matmul(banks[ba][:], lhsT=va[:, k, mi * 128:(mi + 1) * 128], rhs=gin[:, k, :],
                                                                                     start=(k == 0), stop=(k == KC - 1)), r=[bw[sl], bgin], w=[bbk[ba]])
                            self.tt(h_sb[:, m, :], h_sb[:, m, :], banks[ba][:], ALU.add, [bh, bbk[ba]], [bh])
                self.rmsnorm(h_sb[:], bh, self.c_gffn[:, layer * KC:(layer + 1) * KC], hn[:], bhn, sq, bmid, rstd[:], brstd, ps_ss[:], bpss)
                for cb in range(DFF // 256):
                    sl = nslot()
                    vg = wbuf[sl][:, 0:KC * 256].rearrange("p (k n) -> p k n", k=KC)
                    vu = wbuf[sl][:, KC * 256:2 * KC * 256].rearrange("p (k n) -> p k n", k=KC)
                    self.wload(self.w_gate[layer], vg, KC, cb * 256, 256, sl, bw[sl])
                    self.wload(self.w_up[layer], vu, KC, cb * 256, 256, sl, bw[sl])
                    for mi in range(2):
                        jj = cb * 2 + mi
                        bg, bu = nbank(), nbank()
                        for k in range(KC):
                            self.pe(lambda e, k=k, vg=vg, mi=mi, bg=bg: e.matmul(banks[bg][:], lhsT=vg[:, k, mi * 128:(mi + 1) * 128], rhs=hn[:, k, :],
                                                                                 start=(k == 0), stop=(k == KC - 1)), r=[bw[sl], bhn], w=[bbk[bg]])
                        for k in range(KC):
                            self.pe(lambda e, k=k, vu=vu, mi=mi, bu=bu: e.matmul(banks[bu][:], lhsT=vu[:, k, mi * 128:(mi + 1) * 128], rhs=hn[:, k, :],
                                                                                 start=(k == 0), stop=(k == KC - 1)), r=[bw[sl], bhn], w=[bbk[bu]])
                        ti = ntmp()
                        self.act(lambda e, ti=ti, bg=bg: e.activation(out=tmp[ti][:], in_=banks[bg][:], func=AF.Silu), r=[bbk[bg]], w=[btmp[ti]])
                        self.tt(mid[:, jj, :], tmp[ti][:], banks[bu][:], ALU.mult, [btmp[ti], bbk[bu]], [bmid])
                for cb in range(D // 256):
                    sl = nslot()
                    vd = wbuf[sl][:, 0:JC * 256].rearrange("p (k n) -> p k n", k=JC)
                    self.wload(self.w_down[layer], vd, JC, cb * 256, 256, sl, bw[sl])
                    for mi in range(2):
                        m = cb * 2 + mi
                        ba = nbank()
                        for k in range(JC):
                            self.pe(lambda e, k=k, vd=vd, mi=mi, ba=ba: e.matmul(banks[ba][:], lhsT=vd[:, k, mi * 128:(mi + 1) * 128], rhs=mid[:, k, :],
                                                                                 start=(k == 0), stop=(k == JC - 1)), r=[bw[sl], bmid], w=[bbk[ba]])
                        self.tt(h_sb[:, m, :], h_sb[:, m, :], banks[ba][:], ALU.add, [bh, bbk[ba]], [bh])
                self.rmsnorm(h_sb[:], bh, self.c_gple[:, layer * KC:(layer + 1) * KC], hn[:], bhn, sq, bmid, rstd[:], brstd, ps_ss[:], bpss)
                self.load(p32[:], self.pT[layer][:, t0:t0 + T].rearrange("(k p) t -> p k t", p=128), "ld_p", [bp32])
                self.act(lambda e: e.activation(out=pbf[:], in_=p32[:], func=AF.Copy), r=[bp32], w=[bpbf])
                for cb in range(D // 512):
                    sl = nslot()
                    vg = wbuf[sl][:, 0:KC * 512].rearrange("p (k n) -> p k n", k=KC)
                    self.wload(self.w_pg[layer], vg, KC, cb * 512, 512, sl, bw[sl])
                    for mi in range(4):
                        m = cb * 4 + mi
                        bg, bp_ = nbank(), nbank()
                        for k in range(KC):
                            self.pe(lambda e, k=k, vg=vg, mi=mi, bg=bg: e.matmul(banks[bg][:], lhsT=vg[:, k, mi * 128:(mi + 1) * 128], rhs=hn[:, k, :],
                                                                                 start=(k == 0), stop=(k == KC - 1)), r=[bw[sl], bhn], w=[bbk[bg]])
                        for k in range(2):
                            self.pe(lambda e, k=k, m=m, bp_=bp_: e.matmul(banks[bp_][:], lhsT=wpp[:, k, m * 128:(m + 1) * 128], rhs=pbf[:, k, :],
                                                                          start=(k == 0), stop=(k == 1)), r=[bwpp, bpbf], w=[bbk[bp_]])
                        ti = ntmp()
                        self.act(lambda e, ti=ti, bg=bg: e.activation(out=tmp[ti][:], in_=banks[bg][:], func=AF.Sigmoid), r=[bbk[bg]], w=[btmp[ti]])
                        self.tt(tmp[ti][:], tmp[ti][:], banks[bp_][:], ALU.mult, [btmp[ti], bbk[bp_]], [btmp[ti]])
                        self.tt(h_sb[:, m, :], h_sb[:, m, :], tmp[ti][:], ALU.add, [bh, btmp[ti]], [bh])
                P.dma("sp", lambda e, t0=t0: e.dma_start(out=dst_h[:, t0:t0 + T].rearrange("(k p) t -> p k t", p=128), in_=h_sb[:]), "st_h", reads=[bh])
            P.end_phase()

    def phase_s5_pre(self, layer, first_layer):
        cfg, nc, P = self.cfg, self.nc, self.P
        j = layer // 2
        with ExitStack() as st:
            def sb(name, shape, dt=F32):
                return st.enter_context(nc.sbuf_tensor(self.un(name), shape, dt))
            h_sb = sb("h_sb", [128, KC, T]); hn = sb("hn", [128, KC, T], BF16); sq = sb("sq", [128, KC, T], BF16)
            u32 = sb("u32", [128, KC, T]); ubf = sb("ubf", [128, KC, T], BF16)
            wbuf = [sb(f"wbuf{i}", [128, KC, 512], BF16) for i in range(2)]
            rstd = sb("rstd", [128, T])
            ps_ss = st.enter_context(nc.psum_tensor(self.un("ps_ss"), [128, T], F32))
            banks = [st.enter_context(nc.psum_tensor(self.un(f"bk{i}"), [128, T], F32)) for i in range(6)]
            bh, bhn, bsq, bu32, bubf, brstd = (Buf() for _ in range(6)); bpss = PBuf()
            bw = [Buf(), Buf()]; bbk = [PBuf() for _ in range(6)]
            src_h = self.xT if first_layer else self.hT
            wi = 0; bi = 0
            import os
            dbg = os.environ.get("MK_DBG", "z")
            for (s, c, t0) in cfg.tiles:
                self.load(h_sb[:], src_h[:, t0:t0 + T].rearrange("(k p) t -> p k t", p=128), "ld_h", [bh])
                if dbg < "b":
                    continue
                self.rmsnorm(h_sb[:], bh, self.c_gmix[:, layer * KC:(layer + 1) * KC], hn[:], bhn, sq[:], bsq, rstd[:], brstd, ps_ss[:], bpss)
                if dbg < "c":
                    continue
                for cb in range(D // 512):
                    sl = wi % 2; wi += 1
                    self.wload(self.w_in[j], wbuf[sl][:], KC, cb * 512, 512, sl, bw[sl])
                    if dbg == "c1":
                        self.dve(lambda e, sl=sl: e.tensor_copy(out=ubf[:, 0, :], in_=wbuf[sl][:, 0, :]), r=[bw[sl]], w=[bubf])
                        continue
                    for mi in range(4):
                        m = cb * 4 + mi
                        ba = bi % 6; bi += 1
                        for k in range(KC):
                            self.pe(lambda e, k=k, sl=sl, mi=mi, ba=ba: e.matmul(banks[ba][:], lhsT=wbuf[sl][:, k, mi * 128:(mi + 1) * 128], rhs=hn[:, k, :],
                                                                                 start=(k == 0), stop=(k == KC - 1)), r=[bw[sl], bhn], w=[bbk[ba]])
                        self.act(lambda e, m=m, ba=ba: e.activation(out=u32[:, m, :], in_=banks[ba][:], func=AF.Copy), r=[bbk[ba]], w=[bu32])
                        if dbg == "c2":
                            continue
                        self.dve(lambda e, m=m, ba=ba: e.tensor_copy(out=ubf[:, m, :], in_=banks[ba][:]), r=[bbk[ba]], w=[bubf])
                if dbg < "d":
                    continue
                P.dma("sp", lambda e, t0=t0: e.dma_start(out=self.u_scr[:, t0:t0 + T].rearrange("(k p) t -> p k t", p=128), in_=u32[:]), "st_u", reads=[bu32])
                P.dma("sp", lambda e, t0=t0: e.dma_start(out=self.ubf_scr[:, t0:t0 + T].rearrange("(k p) t -> p k t", p=128), in_=ubf[:]), "st_ub", reads=[bubf])
            P.end_phase()

    def phase_s5_tables(self, j):
        nc, P = self.nc, self.P
        with ExitStack() as st:
            def sb(name, shape, dt=F32):
                return st.enter_context(nc.sbuf_tensor(self.un(name), shape, dt))
            a_re = sb("a_re", [128, NTT]); a_im = sb("a_im", [128, NTT]); dtt = sb("dtt", [128, NTT])
            b_re = sb("b_re", [128, NTT, 16]); b_im = sb("b_im", [128, NTT, 16])
            c_re = sb("c_re", [128, NTT, 16]); c_im = sb("c_im", [128, NTT, 16])
            bb_re = sb("bb_re", [128, NTT, 16]); bb_im = sb("bb_im", [128, NTT, 16])
            t1 = sb("t1", [128, NTT]); t2 = sb("t2", [128, NTT]); t3 = sb("t3", [128, NTT])
            sn = sb("sn", [128, NTT]); cs = sb("cs", [128, NTT]); f_re = sb("f_re", [128, NTT]); f_im = sb("f_im", [128, NTT])
            tB = sb("tB", [128, 2 * NTT, 128], BF16); tC = sb("tC", [128, 2 * NTT, 128], BF16)
            mw = [sb(f"mw{i}", [128, 4, 128]) for i in range(2)]
            tps = st.enter_context(nc.psum_tensor(self.un("tps"), [128, 128], F32))
            th = self.s5p[:, 0, :]; th2pi = self.s5p[:, 1, :]; mag = self.s5p[:, 2, :]; lam_re = self.s5p[:, 3, :]; lam_im = self.s5p[:, 4, :]
            B = {n: Buf(n) for n in "a_re a_im dt b_re b_im c_re c_im bb_re bb_im t1 t2 t3 sn cs f_re f_im tB tC mw0 mw1".split()}
            B["tps"] = PBuf("tps")
            bp = self.b_s5p; bc = self.b_const
            L = self.load
            L(a_re[:], self.A_re[j], "t0", [B["a_re"]]); L(a_im[:], self.A_im[j], "t1", [B["a_im"]]); L(dtt[:], self.LOGDT[j], "t2", [B["dt"]])
            L(b_re[:], self.B_re[j], "t3", [B["b_re"]]); L(b_im[:], self.B_im[j], "t4", [B["b_im"]])
            L(c_re[:], self.C_re[j], "t5", [B["c_re"]]); L(c_im[:], self.C_im[j], "t6", [B["c_im"]])

            def tt(out, a, b, op, r, w):
                self.tt(out, a, b, op, [B[x] if isinstance(x, str) else x for x in r], [B[x] if isinstance(x, str) else x for x in w])

            def ts(out, in0, s1, s2, op0, op1, r, w):
                rr = [B[x] if isinstance(x, str) else x for x in r]; ww = [B[x] if isinstance(x, str) else x for x in w]
                if op1 is None:
                    self.dve(lambda e: e.tensor_scalar(out=out, in0=in0, scalar1=s1, scalar2=None, op0=op0), r=rr, w=ww)
                else:
                    self.dve(lambda e: e.tensor_scalar(out=out, in0=in0, scalar1=s1, scalar2=s2, op0=op0, op1=op1), r=rr, w=ww)

            self.act(lambda e: e.activation(out=dtt[:], in_=dtt[:], func=AF.Exp), r=[B["dt"]], w=[B["dt"]])
            tt(t1[:], a_re[:], dtt[:], ALU.mult, ["a_re", "dt"], ["t1"])
            self.act(lambda e: e.activation(out=mag, in_=t1[:], func=AF.Exp), r=[B["t1"]], w=[bp])
            tt(th, a_im[:], dtt[:], ALU.mult, ["a_im", "dt"], [bp])
            ts(th2pi, th, 1.0 / TWO_PI, None, ALU.mult, None, [bp], [bp])
            ts(t2[:], th2pi, MAGIC, MAGIC, ALU.add, ALU.subtract, [bp], ["t2"])
            self.dve(lambda e: e.scalar_tensor_tensor(out=t2[:], in0=t2[:], scalar=-TWO_PI, in1=th, op0=ALU.mult, op1=ALU.add), r=[B["t2"], bp], w=[B["t2"]])
            self.act(lambda e: e.activation(out=sn[:], in_=t2[:], func=AF.Sin), r=[B["t2"]], w=[B["sn"]])
            self.act(lambda e: e.activation(out=t3[:], in_=t2[:], func=AF.Abs), r=[B["t2"]], w=[B["t3"]])
            self.act(lambda e: e.activation(out=cs[:], in_=t3[:], func=AF.Sin, bias=math.pi / 2, scale=-1.0), r=[B["t3"]], w=[B["cs"]])
            tt(lam_re, mag, cs[:], ALU.mult, [bp, "cs"], [bp])
            tt(lam_im, mag, sn[:], ALU.mult, [bp, "sn"], [bp])
            ts(t1[:], lam_re, -1.0, None, ALU.add, None, [bp], ["t1"])
            tt(t3[:], a_re[:], a_re[:], ALU.mult, ["a_re"], ["t3"])
            tt(f_re[:], a_im[:], a_im[:], ALU.mult, ["a_im"], ["f_re"])
            tt(t3[:], t3[:], f_re[:], ALU.add, ["t3", "f_re"], ["t3"])
            self.dve(lambda e: e.reciprocal(out=t3[:], in_=t3[:]), r=[B["t3"]], w=[B["t3"]])
            tt(f_re[:], t1[:], a_re[:], ALU.mult, ["t1", "a_re"], ["f_re"])
            tt(f_im[:], lam_im, a_im[:], ALU.mult, [bp, "a_im"], ["f_im"])
            tt(f_re[:], f_re[:], f_im[:], ALU.add, ["f_re", "f_im"], ["f_re"])
            tt(f_re[:], f_re[:], t3[:], ALU.mult, ["f_re", "t3"], ["f_re"])
            tt(f_im[:], lam_im, a_re[:], ALU.mult, [bp, "a_re"], ["f_im"])
            tt(t2[:], t1[:], a_im[:], ALU.mult, ["t1", "a_im"], ["t2"])
            tt(f_im[:], f_im[:], t2[:], ALU.subtract, ["f_im", "t2"], ["f_im"])
            tt(f_im[:], f_im[:], t3[:], ALU.mult, ["f_im", "t3"], ["f_im"])
            for c in range(16):
                tt(bb_re[:, :, c], f_re[:], b_re[:, :, c], ALU.mult, ["f_re", "b_re"], ["bb_re"])
                tt(t1[:], f_im[:], b_im[:, :, c], ALU.mult, ["f_im", "b_im"], ["t1"])
                tt(bb_re[:, :, c], bb_re[:, :, c], t1[:], ALU.subtract, ["bb_re", "t1"], ["bb_re"])
                tt(bb_im[:, :, c], f_re[:], b_im[:, :, c], ALU.mult, ["f_re", "b_im"], ["bb_im"])
                tt(t1[:], f_im[:], b_re[:, :, c], ALU.mult, ["f_im", "b_re"], ["t1"])
                tt(bb_im[:, :, c], bb_im[:, :, c], t1[:], ALU.add, ["bb_im", "t1"], ["bb_im"])
            ts(c_im[:], c_im[:], -1.0, None, ALU.mult, None, ["c_im"], ["c_im"])
            self.dve(lambda e: e.memset(mw[0][:], 0.0), w=[B["mw0"]])
            self.dve(lambda e: e.memset(mw[1][:], 0.0), w=[B["mw1"]])
            mi = 0
            NT = NTT // 2
            for d in range(2):
                for kc in range(KC):
                    for ri, (bsrc, csrc, bn, cn) in enumerate(((bb_re, c_re, "bb_re", "c_re"), (bb_im, c_im, "bb_im", "c_im"))):
                        for which, src, sname in ((0, bsrc, bn), (1, csrc, cn)):
                            m = mw[mi % 2]; mn = f"mw{mi % 2}"; mi += 1
                            for gg in range(2):
                                p0 = 64 * gg
                                for q in range(4):
                                    tile_idx = d * NT + 4 * kc + q
                                    self.dve(lambda e, m=m, p0=p0, q=q, gg=gg, src=src, tile_idx=tile_idx:
                                             e.tensor_copy(out=m[p0:p0 + 64, q, 32 * q + 16 * gg: 32 * q + 16 * gg + 16], in_=src[p0:p0 + 64, tile_idx, :]),
                                             r=[B[sname]], w=[B[mn]])
                            for q in range(4):
                                tile_idx = d * NT + 4 * kc + q
                                slot = tile_idx * 2 + ri
                                if which == 1:
                                    self.act(lambda e, m=m, q=q, slot=slot: e.activation(out=tC[:, slot, :], in_=m[:, q, :], func=AF.Copy), r=[B[mn]], w=[B["tC"]])
                                else:
                                    self.pe(lambda e, m=m, q=q: e.transpose(out=tps[:], in_=m[:, q, :], identity=self.ident[:]), r=[B[mn], bc], w=[B["tps"]])
                                    self.act(lambda e, slot=slot: e.activation(out=tB[:, slot, :], in_=tps[:], func=AF.Copy), r=[B["tps"]], w=[B["tB"]])
            P.dma("sp", lambda e: e.dma_start(out=self.tblB, in_=tB[:]), "st_tb", reads=[B["tB"]])
            P.dma("sp", lambda e: e.dma_start(out=self.tblC, in_=tC[:]), "st_tc", reads=[B["tC"]])
            P.end_phase()

    def phase_s5_scan(self, j, final):
        cfg, nc, P = self.cfg, self.nc, self.P
        NT = NTT // 2
        GC1 = 2.0 * math.sqrt(2.0 / math.pi)
        with ExitStack() as st:
            def sb(name, shape, dt=F32):
                return st.enter_context(nc.sbuf_tensor(self.un(name), shape, dt))
            NMAX = max(cfg.N)
            ubf_k = sb("ubf_k", [128, NMAX], BF16); u32_k = sb("u32_k", [128, NMAX])
            tBk = sb("tBk", [128, 2, 8, 128], BF16); tCk = sb("tCk", [128, 2, 8, 128], BF16)
            ST = sb("ST", [128, T]); CT = sb("CT", [128, T]); ang = sb("ang", [128, T]); rb = sb("rb", [128, T])
            ta = sb("ta", [128, T]); tb = sb("tb", [128, T]); tc = sb("tc", [128, T]); td = sb("td", [128, T])
            vr = sb("vr", [128, T]); vi = sb("vi", [128, T])
            car = sb("car", [128, 2]); ctmp = sb("ctmp", [128, 4])
            xr = [sb(f"xr{i}", [128, T], BF16) for i in range(2)]
            xi = [sb(f"xi{i}", [128, T], BF16) for i in range(2)]
            yv = sb("yv", [128, T]); gt = sb("gt", [128, T]); gs = sb("gs", [128, T]); gout = [sb(f"gout{i}", [128, T], BF16) for i in range(2)]
            zps = [st.enter_context(nc.psum_tensor(self.un(f"z{i}"), [128, T], F32)) for i in range(4)]
            yps = [st.enter_context(nc.psum_tensor(self.un(f"y{i}"), [128, T], F32)) for i in range(4)]
            B = {n: Buf(n) for n in "ubf u32 tBk tCk ST CT ang rb ta tb tc td vr vi car ctmp xr0 xr1 xi0 xi1 yv gt gs gout0 gout1".split()}
            for n in "z0 z1 z2 z3 y0 y1 y2 y3".split():
                B[n] = PBuf(n)
            bp, bc = self.b_s5p, self.b_const
            th = self.s5p[:, 0, :]; th2pi = self.s5p[:, 1, :]; mag = self.s5p[:, 2, :]
            zi_ = 0; xi_ = 0; go_ = 0

            def tt(out, a, b, op, r, w):
                self.tt(out, a, b, op, [B[x] if isinstance(x, str) else x for x in r], [B[x] if isinstance(x, str) else x for x in w])

            for s in range(2):
                N = cfg.N[s]; nch = cfg.nch[s]; base = cfg.base[s]
                for kc in range(KC):
                    self.load(ubf_k[:, 0:N], self.ubf_scr[kc * 128:(kc + 1) * 128, base:base + N], "ld_ub", [B["ubf"]])
                    if final:
                        self.load(u32_k[:, 0:N], self.u_scr[kc * 128:(kc + 1) * 128, base:base + N], "ld_u", [B["u32"]])
                    for d in range(2):
                        s0 = (d * NT + 4 * kc) * 2
                        self.load(tBk[:, d, :, :], self.tblB[:, s0:s0 + 8, :], "ld_tb", [B["tBk"]])
                        if final:
                            self.load(tCk[:, d, :, :], self.tblC[:, s0:s0 + 8, :], "ld_tc", [B["tCk"]])
                    for q in range(4):
                        for d in range(2):
                            tile_idx = d * NT + 4 * kc + q
                            thp = th[:, tile_idx:tile_idx + 1]
                            self.dve(lambda e, tile_idx=tile_idx: e.tensor_scalar(out=ang[:], in0=self.iota[:], scalar1=th2pi[:, tile_idx:tile_idx + 1], scalar2=MAGIC,
                                                                                  op0=ALU.mult, op1=ALU.add), r=[bc, bp], w=[B["ang"]])
                            self.dve(lambda e: e.tensor_scalar(out=ang[:], in0=ang[:], scalar1=MAGIC, scalar2=-TWO_PI, op0=ALU.subtract, op1=ALU.mult), r=[B["ang"]], w=[B["ang"]])
                            self.dve(lambda e, thp=thp: e.scalar_tensor_tensor(out=ang[:], in0=self.iota[:], scalar=thp, in1=ang[:], op0=ALU.mult, op1=ALU.add),
                                     r=[bc, bp, B["ang"]], w=[B["ang"]])
                            self.act(lambda e: e.activation(out=ST[:], in_=ang[:], func=AF.Sin), r=[B["ang"]], w=[B["ST"]])
                            self.act(lambda e: e.activation(out=ang[:], in_=ang[:], func=AF.Abs), r=[B["ang"]], w=[B["ang"]])
                            self.act(lambda e: e.activation(out=CT[:], in_=ang[:], func=AF.Sin, bias=math.pi / 2, scale=-1.0), r=[B["ang"]], w=[B["CT"]])
                            self.act(lambda e, tile_idx=tile_idx: e.activation(out=rb[:], in_=self.ones_f[:], func=AF.Copy, scale=mag[:, tile_idx:tile_idx + 1]),
                                     r=[bc, bp], w=[B["rb"]])
                            if final:
                                self.dve(lambda e, s=s, tile_idx=tile_idx: e.tensor_copy(out=car[:], in_=self.ini_all[:, s, :, tile_idx]), r=[self.b_ini], w=[B["car"]])
                            else:
                                self.dve(lambda e: e.memset(car[:], 0.0), w=[B["car"]])
                            rv = (lambda ap: ap) if d == 0 else (lambda ap: ap[:, ::-1])
                            for ci in range(nch):
                                ch = ci if d == 0 else nch - 1 - ci
                                c0 = ch * T
                                zr_t = zps[zi_ % 4]; zrn = f"z{zi_ % 4}"; zi_ += 1
                                zi_t = zps[zi_ % 4]; zin = f"z{zi_ % 4}"; zi_ += 1
                                self.pe(lambda e, zr_t=zr_t, d=d, q=q, c0=c0: e.matmul(zr_t[:], lhsT=tBk[:, d, q * 2, :], rhs=ubf_k[:, c0:c0 + T], start=True, stop=True),
                                        r=[B["tBk"], B["ubf"]], w=[B[zrn]])
                                self.pe(lambda e, zi_t=zi_t, d=d, q=q, c0=c0: e.matmul(zi_t[:], lhsT=tBk[:, d, q * 2 + 1, :], rhs=ubf_k[:, c0:c0 + T], start=True, stop=True),
                                        r=[B["tBk"], B["ubf"]], w=[B[zin]])
                                tt(ta[:], rv(zr_t[:]), CT[:], ALU.mult, [zrn, "CT"], ["ta"])
                                tt(tb[:], rv(zi_t[:]), ST[:], ALU.mult, [zin, "ST"], ["tb"])
                                tt(ta[:], ta[:], tb[:], ALU.add, ["ta", "tb"], ["ta"])
                                tt(tc[:], rv(zi_t[:]), CT[:], ALU.mult, [zin, "CT"], ["tc"])
                                tt(td[:], rv(zr_t[:]), ST[:], ALU.mult, [zrn, "ST"], ["td"])
                                tt(tc[:], tc[:], td[:], ALU.subtract, ["tc", "td"], ["tc"])
                                self.dve(lambda e: e.tensor_tensor_scan(out=vr[:], data0=rb[:], data1=ta[:], initial=car[:, 0:1], op0=ALU.mult, op1=ALU.add),
                                         r=[B["rb"], B["ta"], B["car"]], w=[B["vr"]])
                                self.dve(lambda e: e.tensor_tensor_scan(out=vi[:], data0=rb[:], data1=tc[:], initial=car[:, 1:2], op0=ALU.mult, op1=ALU.add),
                                         r=[B["rb"], B["tc"], B["car"]], w=[B["vi"]])
                                if final:
                                    xrt = xr[xi_ % 2]; xrn = f"xr{xi_ % 2}"; xit = xi[xi_ % 2]; xin = f"xi{xi_ % 2}"; xi_ += 1
                                    tt(ta[:], vr[:], CT[:], ALU.mult, ["vr", "CT"], ["ta"])
                                    tt(tb[:], vi[:], ST[:], ALU.mult, ["vi", "ST"], ["tb"])
                                    tt(rv(xrt[:]), ta[:], tb[:], ALU.subtract, ["ta", "tb"], [xrn])
                                    tt(car[:, 0:1], ta[:, T - 1:T], tb[:, T - 1:T], ALU.subtract, ["ta", "tb"], ["car"])
                                    tt(tc[:], vi[:], CT[:], ALU.mult, ["vi", "CT"], ["tc"])
                                    tt(td[:], vr[:], ST[:], ALU.mult, ["vr", "ST"], ["td"])
                                    tt(rv(xit[:]), tc[:], td[:], ALU.add, ["tc", "td"], [xin])
                                    tt(car[:, 1:2], tc[:, T - 1:T], td[:, T - 1:T], ALU.add, ["tc", "td"], ["car"])
                                    yb = yps[ch]; ybn = f"y{ch}"
                                    first = (q == 0 and d == 0); last = (q == 3 and d == 1)
                                    self.pe(lambda e, yb=yb, xrt=xrt, d=d, q=q, first=first: e.matmul(yb[:], lhsT=tCk[:, d, q * 2, :], rhs=xrt[:], start=first, stop=False),
                                            r=[B["tCk"], B[xrn]], w=[B[ybn]])
                                    self.pe(lambda e, yb=yb, xit=xit, d=d, q=q, last=last: e.matmul(yb[:], lhsT=tCk[:, d, q * 2 + 1, :], rhs=xit[:], start=False, stop=last),
                                            r=[B["tCk"], B[xin]], w=[B[ybn]])
                                else:
                                    L1 = slice(T - 1, T)
                                    tt(ctmp[:, 0:1], vr[:, L1], CT[:, L1], ALU.mult, ["vr", "CT"], ["ctmp"])
                                    tt(ctmp[:, 1:2], vi[:, L1], ST[:, L1], ALU.mult, ["vi", "ST"], ["ctmp"])
                                    tt(ctmp[:, 2:3], vi[:, L1], CT[:, L1], ALU.mult, ["vi", "CT"], ["ctmp"])
                                    tt(ctmp[:, 3:4], vr[:, L1], ST[:, L1], ALU.mult, ["vr", "ST"], ["ctmp"])
                                    tt(car[:, 0:1], ctmp[:, 0:1], ctmp[:, 1:2], ALU.subtract, ["ctmp"], ["car"])
                                    tt(car[:, 1:2], ctmp[:, 2:3], ctmp[:, 3:4], ALU.add, ["ctmp"], ["car"])
                            if not final:
                                self.dve(lambda e, s=s, tile_idx=tile_idx: e.tensor_copy(out=self.sloc[:, s, :, tile_idx], in_=car[:]), r=[B["car"]], w=[self.b_sloc])
                    if final:
                        for ch in range(nch):
                            c0 = ch * T
                            dcol = self.c_dsk[:, j * KC + kc: j * KC + kc + 1]
                            self.dve(lambda e, ch=ch, c0=c0, dcol=dcol: e.scalar_tensor_tensor(out=yv[:], in0=u32_k[:, c0:c0 + T], scalar=dcol, in1=yps[ch][:],
                                                                                               op0=ALU.mult, op1=ALU.add), r=[B["u32"], bc, B[f"y{ch}"]], w=[B["yv"]])
                            tt(gt[:], yv[:], yv[:], ALU.mult, ["yv"], ["gt"])
                            self.dve(lambda e: e.tensor_scalar(out=gt[:], in0=gt[:], scalar1=0.044715, scalar2=1.0, op0=ALU.mult, op1=ALU.add), r=[B["gt"]], w=[B["gt"]])
                            tt(gt[:], gt[:], yv[:], ALU.mult, ["gt", "yv"], ["gt"])
                            self.act(lambda e: e.activation(out=gs[:], in_=gt[:], func=AF.Sigmoid, scale=GC1), r=[B["gt"]], w=[B["gs"]])
                            go = gout[go_ % 2]; gon = f"gout{go_ % 2}"; go_ += 1
                            tt(go[:], yv[:], gs[:], ALU.mult, ["yv", "gs"], [gon])
                            r0 = kc * 128
                            P.dma("sp", lambda e, go=go, r0=r0, t0=base + c0: e.dma_start(out=self.g_scr[r0:r0 + 128, t0:t0 + T], in_=go[:]), "st_" + gon, reads=[B[gon]])
            P.end_phase()

    def phase_s5_exchange(self):
        cfg, nc, P = self.cfg, self.nc, self.P
        NC = cfg.nc_
        NT = NTT // 2
        with ExitStack() as st:
            def sb(name, shape, dt=F32):
                return st.enter_context(nc.sbuf_tensor(self.un(name), shape, dt))
            G = sb("G", [128, NC, 512])
            pw = sb("pw", [128, 2, NTT]); LN = sb("LN", [128, 2, 2, NTT]); sqa = sb("sqa", [128, 3, NTT])
            H = [sb(f"H{i}", [128, 2, NT]) for i in range(2)]
            e1 = sb("e1", [128, NT]); e2 = sb("e2", [128, NT])
            bG, bpw, bLN, bsq, be1, be2 = (Buf() for _ in range(6)); bH = [Buf(), Buf()]
            bsrc, bdst = Buf(), Buf()
            bp, bc = self.b_s5p, self.b_const
            P.dma("sp", lambda e: e.dma_start(out=self.ccS_src.ap(), in_=self.sloc[:].rearrange("p s r n -> p (s r n)")), "st_cc", reads=[self.b_sloc], writes=[bsrc])
            if NC > 1:
                P.dma("pool", lambda e: e.collective_compute("AllGather", ALU.bypass, replica_groups=[list(range(NC))],
                                                             ins=[self.ccS_src.ap().opt()], outs=[self.ccS_dst.ap().opt()]), "cc", reads=[bsrc], writes=[bdst], inc=1)
            else:
                P.dma("sp", lambda e: e.dma_start(out=self.ccS_dst.ap(), in_=self.ccS_src.ap()), "cc1", reads=[bsrc], writes=[bdst])
            self.load(G[:], self.ccS_dst.ap().rearrange("(r p) c -> p r c", p=128), "ld_G", [bG], r=[bdst])
            self.dve(lambda e: e.tensor_copy(out=pw[:], in_=self.s5p[:, 3:5, :]), r=[bp], w=[bpw])
            l0 = int(round(math.log2(cfg.N[0]))); l1 = int(round(math.log2(cfg.N[1])))
            assert 2 ** l0 == cfg.N[0] and 2 ** l1 == cfg.N[1]
            for k in range(1, max(l0, l1) + 1):
                self.tt(sqa[:, 0, :], pw[:, 0, :], pw[:, 0, :], ALU.mult, [bpw], [bsq])
                self.tt(sqa[:, 1, :], pw[:, 1, :], pw[:, 1, :], ALU.mult, [bpw], [bsq])
                self.tt(sqa[:, 2, :], pw[:, 0, :], pw[:, 1, :], ALU.mult, [bpw], [bsq])
                self.tt(pw[:, 0, :], sqa[:, 0, :], sqa[:, 1, :], ALU.subtract, [bsq], [bpw])
                self.dve(lambda e: e.tensor_scalar(out=pw[:, 1, :], in0=sqa[:, 2, :], scalar1=2.0, scalar2=None, op0=ALU.mult), r=[bsq], w=[bpw])
                for s in range(2):
                    if k == (l0, l1)[s]:
                        self.dve(lambda e, s=s: e.tensor_copy(out=LN[:, s, :, :], in_=pw[:]), r=[bpw], w=[bLN])
            for s in range(2):
                for d in range(2):
                    cs_ = slice(d * NT, (d + 1) * NT)
                    pp = 0
                    self.dve(lambda e: e.memset(H[0][:], 0.0), w=[bH[0]])
                    order = list(range(NC)) if d == 0 else list(range(NC - 1, -1, -1))
                    for n_, r_ in enumerate(order):
                        oh = self.c_oh[:, r_:r_ + 1]
                        for ri in range(2):
                            if n_ == 0:
                                self.dve(lambda e, s=s, ri=ri, pp=pp, oh=oh, cs_=cs_: e.tensor_scalar(out=self.ini_all[:, s, ri, cs_], in0=H[pp][:, ri, :], scalar1=oh, scalar2=None,
                                                                                                      op0=ALU.mult), r=[bH[pp], bc], w=[self.b_ini])
                            else:
                                self.dve(lambda e, s=s, ri=ri, pp=pp, oh=oh, cs_=cs_: e.scalar_tensor_tensor(out=self.ini_all[:, s, ri, cs_], in0=H[pp][:, ri, :], scalar=oh,
                                                                                                             in1=self.ini_all[:, s, ri, cs_], op0=ALU.mult, op1=ALU.add),
                                         r=[bH[pp], bc, self.b_ini], w=[self.b_ini])
                        Hn = H[1 - pp]; Ho = H[pp]
                        Lr = LN[:, s, 0, cs_]; Li = LN[:, s, 1, cs_]
                        Sr = G[:, r_, s * 256 + d * NT: s * 256 + d * NT + NT]
                        Si = G[:, r_, s * 256 + 128 + d * NT: s * 256 + 128 + d * NT + NT]
                        self.tt(e1[:], Lr, Ho[:, 0, :], ALU.mult, [bLN, bH[pp]], [be1])
                        self.tt(e2[:], Li, Ho[:, 1, :], ALU.mult, [bLN, bH[pp]], [be2])
                        self.tt(e1[:], e1[:], e2[:], ALU.subtract, [be1, be2], [be1])
                        self.tt(Hn[:, 0, :], e1[:], Sr, ALU.add, [be1, bG], [bH[1 - pp]])
                        self.tt(e1[:], Lr, Ho[:, 1, :], ALU.mult, [bLN, bH[pp]], [be1])
                        self.tt(e2[:], Li, Ho[:, 0, :], ALU.mult, [bLN, bH[pp]], [be2])
                        self.tt(e1[:], e1[:], e2[:], ALU.add, [be1, be2], [be1])
                        self.tt(Hn[:, 1, :], e1[:], Si, ALU.add, [be1, bG], [bH[1 - pp]])
                        pp = 1 - pp
                    if pp != 0:
                        pass
            P.end_phase()


    def phase_na_qkv(self, layer, first_layer):
        cfg, nc, P = self.cfg, self.nc, self.P
        j = layer // 2
        with ExitStack() as st:
            def sb(name, shape, dt=F32):
                return st.enter_context(nc.sbuf_tensor(self.un(name), shape, dt))
            h_sb = sb("h_sb", [128, KC, T]); hn = sb("hn", [128, KC, T], BF16); sq = sb("sq", [128, KC, T], BF16)
            wbuf = [sb(f"wbuf{i}", [128, KC, 512], BF16) for i in range(2)]
            qk = [sb("qn", [128, NH, T], BF16), sb("kn", [128, NH, T], BF16)]
            vtok = sb("vtok", [128, 4, D], BF16)
            sq1 = [sb(f"sq1_{i}", [128, T], BF16) for i in range(2)]
            rr = [sb(f"rr{i}", [128, T]) for i in range(2)]
            rstd = sb("rstd", [128, T])
            ps_ss = st.enter_context(nc.psum_tensor(self.un("ps_ss"), [128, T], F32))
            banks = [st.enter_context(nc.psum_tensor(self.un(f"bk{i}"), [128, T], F32)) for i in range(5)]
            hsb = [st.enter_context(nc.psum_tensor(self.un(f"hs{i}"), [128, T], F32)) for i in range(2)]
            bh, bhn, bsq, brstd, bvt = (Buf() for _ in range(5)); bpss = PBuf()
            bqk = [Buf(), Buf()]; bw = [Buf(), Buf()]; bsq1 = [Buf(), Buf()]; brr = [Buf(), Buf()]
            bbk = [PBuf() for _ in range(5)]; bhs = [PBuf(), PBuf()]
            bc = self.b_const
            src_h = self.xT if first_layer else self.hT
            gains = (self.c_qg[:, j:j + 1], self.c_kg[:, j:j + 1])
            ccK = [self.ccK_src[s_].ap().rearrange("p (h a t) -> p h a t", h=NH, a=2) for s_ in range(2)]
            ccV = [self.ccV_src[s_].ap().rearrange("p (a b d) -> p a b d", a=2, b=2) for s_ in range(2)]
            wi = 0; bi = 0; hi = 0
            for (s, c, t0) in cfg.tiles:
                nch = cfg.nch[s]
                self.load(h_sb[:], src_h[:, t0:t0 + T].rearrange("(k p) t -> p k t", p=128), "ld_h", [bh])
                self.rmsnorm(h_sb[:], bh, self.c_gmix[:, layer * KC:(layer + 1) * KC], hn[:], bhn, sq[:], bsq, rstd[:], brstd, ps_ss[:], bpss)
                for part in range(2):
                    dst = qk[part]; bdst = bqk[part]
                    for cb in range(D // 512):
                        sl = wi % 2; wi += 1
                        self.wload(self.w_qkv[j], wbuf[sl][:], KC, part * D + cb * 512, 512, sl, bw[sl])
                        for mi in range(4):
                            head = cb * 4 + mi
                            ba = bi % 5; bi += 1
                            hh = hi % 2; hi += 1
                            for k in range(KC):
                                self.pe(lambda e, k=k, sl=sl, mi=mi, ba=ba: e.matmul(banks[ba][:], lhsT=wbuf[sl][:, k, mi * 128:(mi + 1) * 128], rhs=hn[:, k, :],
                                                                                     start=(k == 0), stop=(k == KC - 1)), r=[bw[sl], bhn], w=[bbk[ba]])
                            self.act(lambda e, hh=hh, ba=ba: e.activation(out=sq1[hh][:], in_=banks[ba][:], func=AF.Square), r=[bbk[ba]], w=[bsq1[hh]])
                            self.pe(lambda e, hh=hh: e.matmul(hsb[hh][:], lhsT=self.ones_b[:], rhs=sq1[hh][:], start=True, stop=True), r=[bsq1[hh], bc], w=[bhs[hh]])
                            if part == 0:
                                self.act(lambda e, hh=hh: e.activation(out=rr[hh][:], in_=hsb[hh][:], func=AF.Sqrt, bias=128.0 * EPS, scale=1.0), r=[bhs[hh]], w=[brr[hh]])
                            else:
                                self.act(lambda e, hh=hh: e.activation(out=rr[hh][:], in_=hsb[hh][:], func=AF.Sqrt, bias=EPS, scale=1.0 / 128.0), r=[bhs[hh]], w=[brr[hh]])
                            self.dve(lambda e, hh=hh: e.reciprocal(out=rr[hh][:], in_=rr[hh][:]), r=[brr[hh]], w=[brr[hh]])
                            self.dve(lambda e, hh=hh, ba=ba, head=head, dst=dst, part=part: e.scalar_tensor_tensor(out=dst[:, head, :], in0=banks[ba][:], scalar=gains[part],
                                                                                                                    in1=rr[hh][:], op0=ALU.mult, op1=ALU.mult),
                                     r=[bbk[ba], brr[hh], bc], w=[bdst])
                ev = 0
                for cb in range(D // 512):
                    sl = wi % 2; wi += 1
                    self.wload(self.w_qkv[j], wbuf[sl][:], KC, 2 * D + cb * 512, 512, sl, bw[sl])
                    for tb in range(4):
                        ba = bi % 5; bi += 1
                        for k in range(KC):
                            self.pe(lambda e, k=k, sl=sl, tb=tb, ba=ba: e.matmul(banks[ba][:], lhsT=hn[:, k, tb * 128:(tb + 1) * 128], rhs=wbuf[sl][:, k, :],
                                                                                 start=(k == 0), stop=(k == KC - 1)), r=[bw[sl], bhn], w=[bbk[ba]])
                        if ev % 2 == 0:
                            self.act(lambda e, tb=tb, cb=cb, ba=ba: e.activation(out=vtok[:, tb, cb * 512:(cb + 1) * 512], in_=banks[ba][:], func=AF.Copy), r=[bbk[ba]], w=[bvt])
                        else:
                            self.dve(lambda e, tb=tb, cb=cb, ba=ba: e.tensor_copy(out=vtok[:, tb, cb * 512:(cb + 1) * 512], in_=banks[ba][:]), r=[bbk[ba]], w=[bvt])
                        ev += 1
                tk0 = (4 + 8 * c) * GRID_W
                P.dma("sp", lambda e, t0=t0: e.dma_start(out=self.q_scr[:, :, t0:t0 + T], in_=qk[0][:]), "st_q", reads=[bqk[0]])
                P.dma("sp", lambda e, s=s, tk0=tk0: e.dma_start(out=self.kT_scr[s][:, :, tk0:tk0 + T], in_=qk[1][:]), "st_k", reads=[bqk[1]])
                P.dma("sp", lambda e, s=s, tk0=tk0: e.dma_start(out=self.v_scr[s][tk0:tk0 + T, :].rearrange("(b p) d -> p b d", p=128), in_=vtok[:]), "st_v", reads=[bvt])
                if c == 0:
                    P.dma("sp", lambda e, s=s: e.dma_start(out=ccK[s][:, :, 0, :], in_=qk[1][:, :, 0:256]), "st_ck0", reads=[bqk[1]])
                    P.dma("sp", lambda e, s=s: e.dma_start(out=ccV[s][:, 0], in_=vtok[:, 0:2, :]), "st_cv0", reads=[bvt])
                if c == nch - 1:
                    P.dma("sp", lambda e, s=s: e.dma_start(out=ccK[s][:, :, 1, :], in_=qk[1][:, :, 256:512]), "st_ck1", reads=[bqk[1]])
                    P.dma("sp", lambda e, s=s: e.dma_start(out=ccV[s][:, 1], in_=vtok[:, 2:4, :]), "st_cv1", reads=[bvt])
            P.end_phase()

    def phase_na_exchange(self):
        cfg, nc, P = self.cfg, self.nc, self.P
        NC = cfg.nc_
        with ExitStack() as st:
            def sb(name, shape, dt=F32):
                return st.enter_context(nc.sbuf_tensor(self.un(name), shape, dt))
            G = sb("G", [128, NC, 4096], BF16)
            acc = [sb(f"acc{i}", [128, 4096], BF16) for i in range(2)]
            bG = Buf(); bacc = [Buf(), Buf()]; bc = self.b_const
            bdK, bdV = [Buf(), Buf()], [Buf(), Buf()]
            for s_ in range(2):
                if NC > 1:
                    P.dma("pool", lambda e, s_=s_: e.collective_compute("AllGather", ALU.bypass, replica_groups=[list(range(NC))],
                                                                        ins=[self.ccK_src[s_].ap().opt()], outs=[self.ccK_dst[s_].ap().opt()]), f"ccK{s_}", writes=[bdK[s_]], inc=1)
                    P.dma("pool", lambda e, s_=s_: e.collective_compute("AllGather", ALU.bypass, replica_groups=[list(range(NC))],
                                                                        ins=[self.ccV_src[s_].ap().opt()], outs=[self.ccV_dst[s_].ap().opt()]), f"ccV{s_}", writes=[bdV[s_]], inc=1)
                else:
                    P.dma("sp", lambda e, s_=s_: e.dma_start(out=self.ccK_dst[s_].ap(), in_=self.ccK_src[s_].ap()), f"ccK1{s_}", writes=[bdK[s_]])
                    P.dma("sp", lambda e, s_=s_: e.dma_start(out=self.ccV_dst[s_].ap(), in_=self.ccV_src[s_].ap()), f"ccV1{s_}", writes=[bdV[s_]])
            gK = [self.ccK_dst[s_].ap().rearrange("(r p) (h a t) -> p r h a t", p=128, h=NH, a=2) for s_ in range(2)]
            gV = [self.ccV_dst[s_].ap().rearrange("(r p) (a b d) -> p r a b d", p=128, a=2, b=2) for s_ in range(2)]
            ai = 0
            for s in range(2):
                R = cfg.R[s]
                for halo in range(2):
                    srcpart = 1 - halo
                    oh0 = (1 + halo) * NC
                    tok0 = 0 if halo == 0 else (R + 4) * GRID_W
                    for which in range(2):
                        for r_ in range(NC):
                            if which == 0:
                                self.load(G[:, r_, :].rearrange("p (h t) -> p h t", h=NH), gK[s][:, r_, :, srcpart, :], "ld_G", [bG], r=[bdK[s]])
                            else:
                                self.load(G[:, r_, :].rearrange("p (b d) -> p b d", b=2), gV[s][:, r_, srcpart], "ld_G", [bG], r=[bdV[s]])
                        a = acc[ai % 2]; ba_ = bacc[ai % 2]; ai += 1
                        for r_ in range(NC):
                            oh = self.c_oh[:, oh0 + r_: oh0 + r_ + 1]
                            if r_ == 0:
                                self.dve(lambda e, a=a, r_=r_, oh=oh: e.tensor_scalar(out=a[:], in0=G[:, r_, :], scalar1=oh, scalar2=None, op0=ALU.mult), r=[bG, bc], w=[ba_])
                            else:
                                self.dve(lambda e, a=a, r_=r_, oh=oh: e.scalar_tensor_tensor(out=a[:], in0=G[:, r_, :], scalar=oh, in1=a[:], op0=ALU.mult, op1=ALU.add),
                                         r=[bG, bc, ba_], w=[ba_])
                        if which == 0:
                            P.dma("sp", lambda e, a=a, s=s, tok0=tok0: e.dma_start(out=self.kT_scr[s][:, :, tok0:tok0 + 256], in_=a[:].rearrange("p (h t) -> p h t", h=NH)),
                                  f"st_hk{ai % 2}", reads=[ba_])
                        else:
                            P.dma("sp", lambda e, a=a, s=s, tok0=tok0: e.dma_start(out=self.v_scr[s][tok0:tok0 + 256, :].rearrange("(b p) d -> p b d", p=128),
                                                                                  in_=a[:].rearrange("p (b d) -> p b d", b=2)), f"st_hv{ai % 2}", reads=[ba_])
            P.end_phase()

    def phase_na_attn(self, layer):
        cfg, nc, P = self.cfg, self.nc, self.P
        j = layer // 2
        with ExitStack() as st:
            def sb(name, shape, dt=F32):
                return st.enter_context(nc.sbuf_tensor(self.un(name), shape, dt))
            Kw = sb("Kw", [128, NH, 1024], BF16); Vw = sb("Vw", [128, 8, D], BF16); Qt = sb("Qt", [128, NH, T], BF16)
            bint = sb("bint", [128, NH, 5, 128]); bsp = sb("bsp", [128, 8, 6, 128])
            ao = sb("ao", [128, NH, T], BF16)
            ee = [sb(f"ee{i}", [128, 6, 128]) for i in range(2)]
            pp = [sb(f"pp{i}", [128, 6, 128], BF16) for i in range(2)]
            rs = [sb(f"rs{i}", [128, 128]) for i in range(2)]
            sc = [st.enter_context(nc.psum_tensor(self.un(f"sc{i}"), [128, 8, 128], F32)) for i in range(2)]
            osb = [st.enter_context(nc.psum_tensor(self.un(f"os{i}"), [128, 4, 128], F32)) for i in range(2)]
            bK, bV, bQ, bbi, bbs, bao = (Buf() for _ in range(6))
            bee = [Buf(), Buf()]; bpp = [Buf(), Buf()]; brs = [Buf(), Buf()]
            bsc = [PBuf(), PBuf()]; bos = [PBuf(), PBuf()]
            bc = self.b_const
            self.load(bint[:], self.BIAS_INT[j], "ld_bi", [bbi])
            it = 0
            for (s, c, t0) in cfg.tiles:
                R = cfg.R[s]
                k0 = 8 * c * GRID_W
                self.load(Kw[:], self.kT_scr[s][:, :, k0:k0 + 1024], "ld_K", [bK])
                self.load(Vw[:], self.v_scr[s][k0:k0 + 1024, :].rearrange("(b p) d -> p b d", p=128), "ld_V", [bV])
                self.load(Qt[:], self.q_scr[:, :, t0:t0 + T], "ld_Q", [bQ])
                for pi in range(4):
                    m = 4 * c + pi
                    if m == 0:
                        var, offs = 0, [0, 1, 2, 3, 4, 5]
                    elif m == 1:
                        var, offs = 1, [0, 1, 2, 3, 4]
                    elif m == R // 2 - 2:
                        var, offs = 2, [0, 1, 2, 3, 4]
                    elif m == R // 2 - 1:
                        var, offs = 3, [-1, 0, 1, 2, 3, 4]
                    else:
                        var, offs = None, [0, 1, 2, 3, 4]
                    rel = [pi + o for o in offs]
                    assert min(rel) >= 0 and max(rel) <= 7
                    ns = len(rel)
                    for h in range(NH):
                        if var is not None and h % 8 == 0:
                            self.load(bsp[:], self.BIAS_SP[j, s, var, :, h:h + 8], "ld_bs", [bbs])
                        i2 = it % 2; it += 1
                        for si, rb_ in enumerate(rel):
                            self.pe(lambda e, i2=i2, si=si, rb_=rb_, h=h, pi=pi: e.matmul(sc[i2][:, si, :], lhsT=Kw[:, h, rb_ * 128:(rb_ + 1) * 128], rhs=Qt[:, h, pi * 128:(pi + 1) * 128],
                                                                                          start=True, stop=True), r=[bK, bQ], w=[bsc[i2]])
                        if var is None:
                            bsrc = bint[:, h]; bb_ = bbi
                        else:
                            bsrc = bsp[:, h % 8]; bb_ = bbs
                        self.tt(ee[i2][:, 0:4, :], sc[i2][:, 0:4, :], bsrc[:, 0:4, :], ALU.add, [bsc[i2], bb_], [bee[i2]])
                        self.tt(ee[i2][:, 4:ns, :], sc[i2][:, 4:ns, :], bsrc[:, 4:ns, :], ALU.add, [bsc[i2], bb_], [bee[i2]])
                        self.act(lambda e, i2=i2, ns=ns: e.activation(out=pp[i2][:, 0:ns, :], in_=ee[i2][:, 0:ns, :], func=AF.Exp), r=[bee[i2]], w=[bpp[i2]])
                        for si, rb_ in enumerate(rel):
                            self.pe(lambda e, i2=i2, si=si, rb_=rb_, h=h, ns=ns: e.matmul(osb[i2][:, 0, :], lhsT=Vw[:, rb_, h * 128:(h + 1) * 128], rhs=pp[i2][:, si, :],
                                                                                          start=(si == 0), stop=(si == ns - 1)), r=[bV, bpp[i2]], w=[bos[i2]])
                        for si, rb_ in enumerate(rel):
                            self.pe(lambda e, i2=i2, si=si, ns=ns: e.matmul(osb[i2][:, 1, :], lhsT=self.ones_b[:], rhs=pp[i2][:, si, :],
                                                                            start=(si == 0), stop=(si == ns - 1)), r=[bc, bpp[i2]], w=[bos[i2]])
                        self.dve(lambda e, i2=i2: e.reciprocal(out=rs[i2][:], in_=osb[i2][:, 1, :]), r=[bos[i2]], w=[brs[i2]])
                        self.tt(ao[:, h, pi * 128:(pi + 1) * 128], osb[i2][:, 0, :], rs[i2][:], ALU.mult, [bos[i2], brs[i2]], [bao])
                P.dma("sp", lambda e, t0=t0: e.dma_start(out=self.g_scr[:, t0:t0 + T].rearrange("(k p) t -> p k t", p=128), in_=ao[:]), "st_ao", reads=[bao])
            P.end_phase()


def build_program(cfg, depth=4):
    import os
    b = Builder(cfg)
    stop = int(os.environ.get("MK_STOP", "1000"))
    phases = [lambda: b.phase_consts()]
    for layer in range(depth):
        j = layer // 2
        first = (layer == 0); last = (layer == depth - 1)
        if layer % 2 == 0:
            phases += [lambda layer=layer, first=first: b.phase_s5_pre(layer, first),
                       lambda j=j: b.phase_s5_tables(j),
                       lambda j=j: b.phase_s5_scan(j, final=False),
                       lambda: b.phase_s5_exchange(),
                       lambda j=j: b.phase_s5_scan(j, final=True),
                       lambda layer=layer, first=first, last=last: b.phase_rowlocal(layer, "s5", first, last)]
        else:
            phases += [lambda layer=layer, first=first: b.phase_na_qkv(layer, first),
                       lambda: b.phase_na_exchange(),
                       lambda layer=layer: b.phase_na_attn(layer),
                       lambda layer=layer, first=first, last=last: b.phase_rowlocal(layer, "na", first, last)]
    for i, ph in enumerate(phases):
        if i >= stop:
            break
        ph()
        print("phase", i + 1, "done; insts so far", b.P.n_inst, flush=True)
    b.outer.close()
    return b.nc, b


def _lay_gp(a):
    return np.ascontiguousarray(a.reshape(2, 64, 2, 64).transpose(2, 3, 0, 1).reshape(128, 128))


def _lay_b(a):
    return np.ascontiguousarray(a.reshape(2, 64, 2, 64, 16).transpose(2, 3, 0, 1, 4).reshape(128, 128, 16))


def _lay_c(a):
    return np.ascontiguousarray(a.reshape(2, 64, 2, 16, 64).transpose(2, 4, 0, 1, 3).reshape(128, 128, 16))


def _bias_block(rpb, rows_total, g0, m, offs, nslots):
    out = np.full((128, NH, nslots, 128), NEG, np.float32)
    kr_i = np.arange(128) // 64; kcol = np.arange(128) % 64
    qr_i = np.arange(128) // 64; qcol = np.arange(128) % 64
    qr = g0 + 2 * m + qr_i
    rs = np.clip(qr - 4, 0, rows_total - 8)
    cs = np.clip(qcol - 8, 0, GRID_W - 16)
    for si, off in enumerate(offs):
        b = m + off
        kr = g0 + 2 * b - 4 + kr_i
        vr = (kr[:, None] >= rs[None, :]) & (kr[:, None] < rs[None, :] + 8) & (kr[:, None] >= 0) & (kr[:, None] < rows_total)
        vc = (kcol[:, None] >= cs[None, :]) & (kcol[:, None] < cs[None, :] + 16)
        valid = vr & vc
        ri = np.clip(kr[:, None] - qr[None, :] + 7, 0, 14)
        ci = np.clip(kcol[:, None] - qcol[None, :] + 15, 0, 30)
        g = rpb[:, ri, ci]
        blk = np.where(valid[None], g, np.float32(NEG)).astype(np.float32)
        out[:, :, si, :] = blk.transpose(1, 0, 2)
    return out


def prep_inputs(inp, cfg):
    NC = cfg.nc_
    N0, N1 = cfg.N
    f = lambda a: np.ascontiguousarray(np.asarray(a, dtype=np.float32))
    gl = lambda g: np.ascontiguousarray(np.asarray(g, np.float32).reshape(4, KC, 128).transpose(2, 0, 1))
    common = {
        "gmix": gl(inp["norm_mix"]), "gffn": gl(inp["norm_ffn"]), "gple": gl(inp["norm_ple"]),
        "s5_w_in": f(inp["s5_w_in"]), "s5_w_glu": f(inp["s5_w_glu"]),
        "A_re": np.stack([_lay_gp(np.asarray(inp["s5_a_re"][j])) for j in range(2)]),
        "A_im": np.stack([_lay_gp(np.asarray(inp["s5_a_im"][j])) for j in range(2)]),
        "LOGDT": np.stack([_lay_gp(np.broadcast_to(np.asarray(inp["s5_log_dt"][j])[:, :, None], (2, 128, 64))) for j in range(2)]),
        "B_re": np.stack([_lay_b(np.asarray(inp["s5_b_re"][j])) for j in range(2)]),
        "B_im": np.stack([_lay_b(np.asarray(inp["s5_b_im"][j])) for j in range(2)]),
        "C_re": np.stack([_lay_c(np.asarray(inp["s5_c_re"][j])) for j in range(2)]),
        "C_im": np.stack([_lay_c(np.asarray(inp["s5_c_im"][j])) for j in range(2)]),
        "DSK": np.ascontiguousarray(np.asarray(inp["s5_d"], np.float32).reshape(2, KC, 128).transpose(0, 2, 1)),
        "attn_w_qkv": f(inp["attn_w_qkv"]), "attn_w_o": f(inp["attn_w_o"]),
        "QG": f(np.asarray(inp["attn_q_norm"]).reshape(2, 128, 1)), "KG": f(np.asarray(inp["attn_k_norm"]).reshape(2, 128, 1)),
        "ffn_w_gate": f(inp["ffn_w_gate"]), "ffn_w_up": f(inp["ffn_w_up"]), "ffn_w_down": f(inp["ffn_w_down"]),
        "ple_w_gate": f(inp["ple_w_gate"]), "ple_w_proj": f(inp["ple_w_proj"]),
        "IOTA": np.broadcast_to(np.arange(1, T + 1, dtype=np.float32), (128, T)).copy(),
        "IDENT": np.eye(128, dtype=np.float32),
    }
    for k in ("A_re", "A_im", "LOGDT", "B_re", "B_im", "C_re", "C_im"):
        common[k] = np.ascontiguousarray(common[k].astype(np.float32))
    rpb = np.asarray(inp["attn_rpb"], np.float32)
    xp = np.asarray(inp["x_prompt"], np.float32)[0]; xs = np.asarray(inp["x_sample"], np.float32)[0]
    pp = np.asarray(inp["p_prompt"], np.float32)[:, 0]; ps_ = np.asarray(inp["p_sample"], np.float32)[:, 0]
    bias_int = np.stack([_bias_block(rpb[j], 64, 16, 4, list(range(5)), 5) for j in range(2)])
    maps = []
    for c in range(NC):
        m = dict(common)
        m["xT"] = np.ascontiguousarray(np.concatenate([xp[c * N0:(c + 1) * N0].T, xs[c * N1:(c + 1) * N1].T], axis=1))
        m["pT"] = np.ascontiguousarray(np.concatenate([pp[:, c * N0:(c + 1) * N0].transpose(0, 2, 1), ps_[:, c * N1:(c + 1) * N1].transpose(0, 2, 1)], axis=2))
        oh = np.zeros((128, 3, NC), np.float32)
        oh[:, 0, c] = 1.0
        if c > 0:
            oh[:, 1, c - 1] = 1.0
        if c < NC - 1:
            oh[:, 2, c + 1] = 1.0
        m["OH"] = oh
        m["BIAS_INT"] = bias_int
        sp = np.zeros((2, 2, 4, 128, NH, 6, 128), np.float32)
        for j in range(2):
            for s in range(2):
                R = cfg.R[s]; rows_total = NC * R; g0 = c * R
                sp[j, s, 0] = _bias_block(rpb[j], rows_total, g0, 0, [0, 1, 2, 3, 4, 5], 6)
                sp[j, s, 1] = _bias_block(rpb[j], rows_total, g0, 1, [0, 1, 2, 3, 4], 6)
                sp[j, s, 2] = _bias_block(rpb[j], rows_total, g0, R // 2 - 2, [0, 1, 2, 3, 4], 6)
                sp[j, s, 3] = _bias_block(rpb[j], rows_total, g0, R // 2 - 1, [-1, 0, 1, 2, 3, 4], 6)
        m["BIAS_SP"] = sp
        maps.append(m)
    return maps


def run(inp, cfg, depth=4):
    nc, b = build_program(cfg, depth)
    maps = prep_inputs(inp, cfg)
    maps = [{k: v for k, v in m.items() if k in b.ext_inputs} for m in maps]
    res = run_bass_kernel_spmd(nc, maps, core_ids=list(range(cfg.nc_)))
    N0, N1 = cfg.N
    yp = np.concatenate([res.results[c]["yT"][:, :N0].T for c in range(cfg.nc_)], axis=0)[None]
    ys = np.concatenate([res.results[c]["yT"][:, N0:].T for c in range(cfg.nc_)], axis=0)[None]
    return np.ascontiguousarray(yp.astype(np.float32)), np.ascontiguousarray(ys.astype(np.float32))


def kernel(**inputs):
    cfg = Cfg(8, 16, 32)
    return run(inputs, cfg, 4)
```
